# Optimizing a Trainium2 kernel written in Bass

```python
import jax, jax.numpy as jnp
from jax import lax
import numpy as np

D_MODEL = 1024
BATCH = 4
SEQ = 8192
DEPTH = 2

HEAD_DIM = 64
POOL_WIDTH = D_MODEL // 4
POOL_WINDOWS = (2, 4, 8, 16)
N_POOL_GROUPS = len(POOL_WINDOWS)
POOL_GC = POOL_WIDTH // N_POOL_GROUPS
ATTN_WIDTH = D_MODEL - POOL_WIDTH
N_ATTN_HEADS = ATTN_WIDTH // HEAD_DIM
DIL_PATTERNS = ((128, 1), (512, 4), (2048, 16))
HEADS_PER_PATTERN = N_ATTN_HEADS // len(DIL_PATTERNS)
ROT_DIM = HEAD_DIM // 4
ROPE_THETA = 500000.0
BLK = 128
D_FF = 4 * D_MODEL
PLE_DIM = 256
EPS = 1e-6

kernel_name = "hybrid_pool_dilated_attn_block"


def rmsnorm(x, g):
    xf = x.astype(jnp.float32)
    y = xf * lax.rsqrt(jnp.mean(xf * xf, axis=-1, keepdims=True) + EPS)
    return (y * g.astype(jnp.float32)).astype(x.dtype)


def rotary_tables(positions, dtype):
    inv_freq = ROPE_THETA ** (-jnp.arange(0, ROT_DIM, 2, dtype=jnp.float32) / ROT_DIM)
    ang = positions.astype(jnp.float32)[..., None] * inv_freq
    return jnp.cos(ang)[:, :, None, :].astype(dtype), jnp.sin(ang)[:, :, None, :].astype(dtype)


def apply_partial_rotary(x, cos, sin):
    half = ROT_DIM // 2
    x1 = x[..., :half]
    x2 = x[..., half:ROT_DIM]
    rot = jnp.concatenate([x1 * cos - x2 * sin, x2 * cos + x1 * sin], axis=-1)
    return jnp.concatenate([rot, x[..., ROT_DIM:]], axis=-1)


def pool_mixer(u, w, scale):
    B, S, _ = u.shape
    ug = u.reshape(B, S, N_POOL_GROUPS, POOL_GC).astype(jnp.float32)
    c = lax.cumsum(ug, axis=1)
    c0 = jnp.pad(c, ((0, 0), (1, 0), (0, 0), (0, 0)))
    t = jnp.arange(S, dtype=jnp.int32)
    win = jnp.array(POOL_WINDOWS, dtype=jnp.int32)
    lo = jnp.maximum(t[:, None] + 1 - win[None, :], 0)
    c_lo = jnp.take_along_axis(c0, lo[None, :, :, None], axis=1)
    cnt = (t[:, None] + 1 - lo).astype(jnp.float32)
    y = ((c - c_lo) / cnt[None, :, :, None] - ug).astype(u.dtype)
    y = jnp.einsum('bsgc,gcd->bsgd', y, w).reshape(B, S, POOL_WIDTH)
    return y * scale


def dilated_window_attention(q, k, v, window, dil):
    B, S, H, Dh = q.shape
    steps = window // dil
    L = -(-S // (dil * BLK)) * BLK
    pad = L * dil - S
    nb = L // BLK

    def to_strided(a):
        a = jnp.pad(a, ((0, 0), (0, pad), (0, 0), (0, 0)))
        a = a.reshape(B, L, dil, H, Dh).transpose(0, 2, 1, 3, 4)
        return a.reshape(B, dil, nb, BLK, H, Dh)

    def with_prev(a):
        prev = jnp.pad(a[:, :, :-1], ((0, 0), (0, 0), (1, 0), (0, 0), (0, 0), (0, 0)))
        return jnp.concatenate([prev, a], axis=3)

    qs = to_strided(q)
    kb = with_prev(to_strided(k))
    vb = with_prev(to_strided(v))
    s = jnp.einsum('brnqhd,brnkhd->brnhqk', qs, kb,
                   preferred_element_type=jnp.float32) * (HEAD_DIM ** -0.5)
    qi = jnp.arange(nb)[:, None] * BLK + jnp.arange(BLK)[None, :]
    ki = jnp.arange(nb)[:, None] * BLK - BLK + jnp.arange(2 * BLK)[None, :]
    dist = qi[:, :, None] - ki[:, None, :]
    valid = (dist >= 0) & (dist <= steps) & (ki[:, None, :] >= 0)
    s = jnp.where(valid[None, None, :, None], s, -jnp.inf)
    m = jnp.max(s, axis=-1, keepdims=True)
    e = jnp.exp(s - m)
    l = jnp.sum(e, axis=-1, keepdims=True)
    o = jnp.einsum('brnhqk,brnkhd->brnqhd', e / l, vb.astype(jnp.float32))
    lse = (m + jnp.log(l))[..., 0]
    o = o.reshape(B, dil, L, H, Dh).transpose(0, 2, 1, 3, 4).reshape(B, L * dil, H, Dh)[:, :S]
    lse = lse.transpose(0, 1, 2, 4, 3).reshape(B, dil, L, H).transpose(0, 2, 1, 3)
    lse = lse.reshape(B, L * dil, H)[:, :S]
    return o, lse


def dilated_mixer(q, k, v):
    outs, lses = [], []
    for g, (window, dil) in enumerate(DIL_PATTERNS):
        sl = slice(g * HEADS_PER_PATTERN, (g + 1) * HEADS_PER_PATTERN)
        o, lse = dilated_window_attention(q[:, :, sl], k[:, :, sl], v[:, :, sl], window, dil)
        outs.append(o)
        lses.append(lse)
    wts = jax.nn.softmax(jnp.stack(lses, axis=0), axis=0)
    o = jnp.concatenate([outs[g] * wts[g][..., None] for g in range(len(DIL_PATTERNS))], axis=2)
    B, S = q.shape[0], q.shape[1]
    return o.reshape(B, S, ATTN_WIDTH).astype(q.dtype)


def setup_inputs(seed: int = 0) -> dict:
    key = jax.random.key(seed)
    ks = jax.random.split(key, 16)
    f32 = jnp.float32
    n_in = POOL_WIDTH + 3 * ATTN_WIDTH
    return {
        "x": jax.random.normal(ks[0], (BATCH, SEQ, D_MODEL), f32),
        "p": jax.random.normal(ks[1], (DEPTH, BATCH, SEQ, PLE_DIM), f32),
        "positions": jnp.broadcast_to(jnp.arange(SEQ, dtype=jnp.int32), (BATCH, SEQ)),
        "norm1": 1.0 + 0.02 * jax.random.normal(ks[2], (DEPTH, D_MODEL), f32),
        "w_in": jax.random.normal(ks[3], (DEPTH, D_MODEL, n_in), f32) * D_MODEL ** -0.5,
        "pool_w": jax.random.normal(ks[4], (DEPTH, N_POOL_GROUPS, POOL_GC, POOL_GC), f32) * POOL_GC ** -0.5,
        "pool_scale": 1.0 + 0.02 * jax.random.normal(ks[5], (DEPTH, POOL_WIDTH), f32),
        "w_out": jax.random.normal(ks[6], (DEPTH, POOL_WIDTH + ATTN_WIDTH, D_MODEL), f32) * (POOL_WIDTH + ATTN_WIDTH) ** -0.5,
        "norm2": 1.0 + 0.02 * jax.random.normal(ks[7], (DEPTH, D_MODEL), f32),
        "w_up": jax.random.normal(ks[8], (DEPTH, D_MODEL, D_FF), f32) * D_MODEL ** -0.5,
        "w_down": jax.random.normal(ks[9], (DEPTH, D_FF, D_MODEL), f32) * D_FF ** -0.5,
        "norm3": 1.0 + 0.02 * jax.random.normal(ks[10], (DEPTH, D_MODEL), f32),
        "w_gate": jax.random.normal(ks[11], (DEPTH, D_MODEL, D_MODEL), f32) * D_MODEL ** -0.5,
        "w_ple": jax.random.normal(ks[12], (DEPTH, PLE_DIM, D_MODEL), f32) * PLE_DIM ** -0.5,
        "final_norm": 1.0 + 0.02 * jax.random.normal(ks[13], (D_MODEL,), f32),
    }


def reference(x, p, positions, norm1, w_in, pool_w, pool_scale, w_out, norm2, w_up, w_down,
              norm3, w_gate, w_ple, final_norm):
    B, S, _ = x.shape
    cos, sin = rotary_tables(positions, x.dtype)
    h = x
    for i in range(DEPTH):
        hn = rmsnorm(h, norm1[i])
        z = hn @ w_in[i]
        u = z[..., :POOL_WIDTH]
        q, k, v = jnp.split(z[..., POOL_WIDTH:], 3, axis=-1)
        q = apply_partial_rotary(q.reshape(B, S, N_ATTN_HEADS, HEAD_DIM), cos, sin)
        k = apply_partial_rotary(k.reshape(B, S, N_ATTN_HEADS, HEAD_DIM), cos, sin)
        v = v.reshape(B, S, N_ATTN_HEADS, HEAD_DIM)
        pool_out = pool_mixer(u, pool_w[i], pool_scale[i])
        attn_out = dilated_mixer(q, k, v)
        h = h + jnp.concatenate([pool_out, attn_out], axis=-1) @ w_out[i]
        hn = rmsnorm(h, norm2[i])
        h = h + jnp.square(jax.nn.relu(hn @ w_up[i])) @ w_down[i]
        gate = jax.nn.sigmoid(rmsnorm(h, norm3[i]) @ w_gate[i])
        h = h + gate * (p[i] @ w_ple[i])
    return rmsnorm(h, final_norm)
```

```python
import numpy as np
from contextlib import ExitStack
import concourse.bass as bass
import concourse.mybir as mybir
from concourse.bass_utils import run_bass_kernel_spmd

F32, BF16, I32 = mybir.dt.float32, mybir.dt.bfloat16, mybir.dt.int32
AF = mybir.ActivationFunctionType
ALU = mybir.AluOpType

T = 512
D = 1024
KC = 8
NIN = 2560
DFF = 4096
EPS = 1e-6
TWO_PI = float(2.0 * np.pi)
PI = float(np.pi)
CW1 = 6.28125
CW2 = float(np.float32(2.0 * np.pi - 6.28125))
CW3 = float(2.0 * np.pi - 6.28125 - float(np.float32(2.0 * np.pi - 6.28125)))
DILS = (1, 4, 16)


class RecIns:
    def __init__(self, ent):
        self.ent = ent

    def then_inc(self, sem, n):
        self.ent[3] = (sem, n)
        return self


class Rec:
    def __init__(self):
        self.prog = []

    def __getattr__(self, name):
        def f(*a, **k):
            ent = [name, a, k, None]
            self.prog.append(ent)
            return RecIns(ent)
        return f


def replay(h, prog):
    for name, a, k, inc in prog:
        ins = getattr(h, name)(*a, **k)
        if inc is not None:
            ins.then_inc(*inc)


class Res:
    __slots__ = ("w", "rs", "px")

    def __init__(self, px=False):
        self.w = None
        self.rs = {}
        self.px = px


class DSem:
    def __init__(self, sem):
        self.sem = sem
        self.cnt = 0


class Eng:
    def __init__(self, B, name, h, strict):
        self.B = B
        self.name = name
        self.h = h
        self.strict = strict
        self.sem = B.new_sem(name)
        self.cnt = 0
        self.seen = {}

    def wait(self, tok):
        sem, val, owner = tok
        if owner is self and not self.strict:
            return
        k = id(sem)
        if self.seen.get(k, 0) >= val:
            return
        self.h.wait_ge(sem, val)
        self.seen[k] = val


class Builder:
    def __init__(self, nc, cfg):
        self.nc = nc
        self.cfg = cfg
        self.es = ExitStack()
        self.nsem = 0
        self.sems = []

    def new_sem(self, name):
        s = self.es.enter_context(self.nc.semaphore(f"{name}{self.nsem}"))
        self.nsem += 1
        self.sems.append(s)
        return s

    def dsem(self, name="d"):
        return DSem(self.new_sem(name))

    def _waits(self, eng, reads, writes):
        for r in reads:
            if r.w is not None:
                eng.wait(r.w)
            if r.px:
                for t in r.rs.values():
                    if t[2] is not eng:
                        eng.wait(t)
        for w in writes:
            if w.w is not None:
                eng.wait(w.w)
            for t in w.rs.values():
                eng.wait(t)

    def _record(self, tok, key, reads, writes):
        for r in reads:
            r.rs[key] = tok
        for w in writes:
            w.w = tok
            w.rs = {}

    def op(self, eng, fn, reads=(), writes=(), sig=True):
        self._waits(eng, reads, writes)
        ins = fn(eng.h)
        if sig:
            eng.cnt += 1
            ins.then_inc(eng.sem, 1)
            tok = (eng.sem, eng.cnt, eng)
        else:
            tok = (eng.sem, eng.cnt + 1, eng)
        self._record(tok, eng.name, reads, writes)
        return ins

    def dma(self, q, out, in_, ds, reads=(), writes=(), **kw):
        self._waits(q, reads, writes)
        ins = q.h.dma_start(out=out, in_=in_, **kw)
        ds.cnt += 16
        ins.then_inc(ds.sem, 16)
        tok = (ds.sem, ds.cnt, None)
        self._record(tok, ("d", id(ds)), reads, writes)
        return tok

    def mm(self, out, lhsT, rhs, start, stop, reads, writes, sig):
        self.op(self.pe, lambda h: h.matmul(out, lhsT, rhs, start=start, stop=stop),
                reads=reads, writes=writes, sig=sig)

    def emit(self):
        nc, cfg = self.nc, self.cfg
        NTL = cfg["ntile"]
        NTOK = NTL * T
        NL = len(cfg["layers"])
        self.NTL, self.NTOK, self.NL = NTL, NTOK, NL
        dt = nc.dram_tensor
        I = "ExternalInput"
        self.xT = dt("xT", [D, NTOK], F32, kind=I).ap()
        self.pT = dt("pT", [NL, 256, NTOK], F32, kind=I).ap()
        self.posr = dt("posr", [8 * NTL, T], I32, kind=I).ap()
        self.invfr = dt("invfr", [8 * NTL, 1], F32, kind=I).ap()
        self.w_in = dt("w_in", [NL, D, NIN], F32, kind=I).ap()
        self.w_out = dt("w_out", [NL, D, D], F32, kind=I).ap()
        self.w_up = dt("w_up", [NL, D, DFF], F32, kind=I).ap()
        self.w_down = dt("w_down", [NL, DFF, D], F32, kind=I).ap()
        self.w_gate = dt("w_gate", [NL, D, D], F32, kind=I).ap()
        self.w_ple = dt("w_ple", [NL, 256, D], F32, kind=I).ap()
        self.wpbd = dt("wpbd", [NL, 128, 256], F32, kind=I).ap()
        self.nrm = dt("nrm", [NL, 128, 24], F32, kind=I).ap()
        self.fnorm = dt("fnorm", [128, 8], F32, kind=I).ap()
        self.pscale = dt("pscale", [NL, 128, 2], F32, kind=I).ap()
        self.cst = dt("cst", [128, 2 + 32], F32, kind=I).ap()
        self.cmat = dt("cmat", [128, 256 + 256 + 256], F32, kind=I).ap()
        self.vtabd = dt("vtab", [128, NTL * 64], F32, kind=I).ap()
        nout = cfg["nout"]
        SCR = cfg.get("scratch_kind", "Internal")
        self.outT = dt("outT", [D, nout * T], F32, kind="ExternalOutput").ap()
        self.hm = dt("hm_s", [D, NTOK], F32, kind=SCR).ap()
        self.hf = dt("hf_s", [D, NTOK], F32, kind=SCR).ap()
        self.h1 = dt("h1_s", [D, NTOK], F32, kind=SCR).ap()
        self.hg = dt("hg_s", [D, NTOK], F32, kind=SCR).ap()
        self.tabd = dt("tab_s", [2, 128, NTOK], F32, kind=SCR).ap()
        self.dres = {}

        with self.es:
            self.pe = Eng(self, "pe", Rec(), False)
            self.act = Eng(self, "act", Rec(), True)
            self.dve = Eng(self, "dve", Rec(), True)
            self.pool = Eng(self, "pool", Rec(), True)
            self.sp = Eng(self, "sp", Rec(), False)

            self.pz2 = self.es.enter_context(nc.psum_tensor("pz2", [128, 1024], F32))
            self.pbank = [self.es.enter_context(nc.psum_tensor(f"pb{i}", [128, 512], F32)) for i in range(6)]
            self.rz = [Res(True), Res(True)]
            self.rb = [Res(True) for _ in range(6)]

            self.phase_tables()
            out_toks = []
            stop = cfg.get("stop_after")
            for li, lay in enumerate(cfg["layers"]):
                self.li = li
                self.lay = lay
                if stop == "tables":
                    break
                self.phase_M()
                if stop == "M":
                    break
                self.phase_F(0)
                self.phase_F(1)
                if stop == "F":
                    break
                out_toks += self.phase_G()
            for t in out_toks:
                self.sp.wait(t)
            with nc.Block() as blk:
                @blk.tensor
                def _(h):
                    replay(h, self.pe.h.prog)

                @blk.scalar
                def _(h):
                    replay(h, self.act.h.prog)

                @blk.vector
                def _(h):
                    replay(h, self.dve.h.prog)

                @blk.gpsimd
                def _(h):
                    replay(h, self.pool.h.prog)

                @blk.sync
                def _(h):
                    replay(h, self.sp.h.prog)

    def dreg(self, name, j):
        k = (name, j)
        if k not in self.dres:
            self.dres[k] = Res()
        return self.dres[k]

    def hview(self, ap, j):
        return ap.rearrange("(c p) n -> p c n", p=128)[:, :, j * T:(j + 1) * T]

    def src_of(self, name):
        return {"x": self.xT, "hm": self.hm, "hf": self.hf, "h1": self.h1, "hg": self.hg}[name]

    def phase_tables(self):
        nc = self.nc
        NTL = self.NTL
        PR = 8 * NTL
        with ExitStack() as es:
            def sb(name, shape, dtype):
                return es.enter_context(nc.sbuf_tensor(name, shape, dtype))
            pi = sb("t_pi", [128, T], I32)
            invf = sb("t_invf", [128, 1], F32)
            ang = sb("t_ang", [128, T], F32)
            kf = sb("t_kf", [128, T], F32)
            ki = sb("t_ki", [128, T], I32)
            y = sb("t_y", [128, T], F32)
            yc = sb("t_yc", [128, T], F32)
            tt = sb("t_t", [128, T], F32)
            so = sb("t_so", [128, T], F32)
            co = sb("t_co", [128, T], F32)
            R = {n: Res() for n in ["pi", "invf", "ang", "kf", "ki", "y", "yc", "tt", "so", "co"]}
            ds = self.dsem("tb")
            ds1, ds2 = self.dsem("tb1"), self.dsem("tb2")
            P = slice(0, PR)
            self.dma(self.sp, pi[P, :], self.posr[:, :], ds1, writes=[R["pi"]])
            self.dma(self.sp, invf[P, :], self.invfr[:, :], ds2, writes=[R["invf"]])
            dv = self.dve
            self.op(dv, lambda h: h.tensor_copy(out=ang[P, :], in_=pi[P, :]), [R["pi"]], [R["ang"]])
            self.op(dv, lambda h: h.tensor_scalar(out=ang[P, :], in0=ang[P, :], scalar1=invf[P, 0:1], scalar2=None, op0=ALU.mult),
                    [R["invf"], R["ang"]], [R["ang"]])
            self.op(dv, lambda h: h.tensor_scalar(out=kf[P, :], in0=ang[P, :], scalar1=1.0 / TWO_PI, scalar2=None, op0=ALU.mult),
                    [R["ang"]], [R["kf"]])
            self.op(dv, lambda h: h.tensor_copy(out=ki[P, :], in_=kf[P, :]), [R["kf"]], [R["ki"]])
            self.op(dv, lambda h: h.tensor_copy(out=kf[P, :], in_=ki[P, :]), [R["ki"]], [R["kf"]])
            self.op(dv, lambda h: h.scalar_tensor_tensor(out=y[P, :], in0=kf[P, :], scalar=-CW1, in1=ang[P, :], op0=ALU.mult, op1=ALU.add),
                    [R["kf"], R["ang"]], [R["y"]])
            self.op(dv, lambda h: h.scalar_tensor_tensor(out=y[P, :], in0=kf[P, :], scalar=-CW2, in1=y[P, :], op0=ALU.mult, op1=ALU.add),
                    [R["kf"], R["y"]], [R["y"]])
            self.op(dv, lambda h: h.scalar_tensor_tensor(out=y[P, :], in0=kf[P, :], scalar=-CW3, in1=y[P, :], op0=ALU.mult, op1=ALU.add),
                    [R["kf"], R["y"]], [R["y"]])

            def wrap(buf, rn):
                self.op(dv, lambda h: h.tensor_scalar(out=tt[P, :], in0=buf[P, :], scalar1=PI, scalar2=TWO_PI, op0=ALU.is_gt, op1=ALU.mult),
                        [R[rn]], [R["tt"]])
                self.op(dv, lambda h: h.tensor_tensor(out=buf[P, :], in0=buf[P, :], in1=tt[P, :], op=ALU.subtract),
                        [R[rn], R["tt"]], [R[rn]])
                self.op(dv, lambda h: h.tensor_scalar(out=tt[P, :], in0=buf[P, :], scalar1=-PI, scalar2=TWO_PI, op0=ALU.is_lt, op1=ALU.mult),
                        [R[rn]], [R["tt"]])
                self.op(dv, lambda h: h.tensor_tensor(out=buf[P, :], in0=buf[P, :], in1=tt[P, :], op=ALU.add),
                        [R[rn], R["tt"]], [R[rn]])
            wrap(y, "y")
            self.op(dv, lambda h: h.tensor_scalar(out=yc[P, :], in0=y[P, :], scalar1=PI / 2, scalar2=None, op0=ALU.add),
                    [R["y"]], [R["yc"]])
            wrap(yc, "yc")
            self.op(self.act, lambda h: h.activation(out=so[P, :], in_=y[P, :], func=AF.Sin), [R["y"]], [R["so"]])
            self.op(self.act, lambda h: h.activation(out=co[P, :], in_=yc[P, :], func=AF.Sin), [R["yc"]], [R["co"]])
            rt = self.dreg("tab", 0)
            cone = sb("t_one", [128, self.NTOK], F32)
            czero = sb("t_zero", [128, self.NTOK], F32)
            rc = Res()
            self.op(self.pool, lambda h: h.memset(cone[:, :], 1.0), [], [rc])
            self.op(self.pool, lambda h: h.memset(czero[:, :], 0.0), [], [rc])
            self.dma(self.sp, self.tabd[0], cone[:, :], ds, reads=[rc], writes=[rt])
            self.dma(self.sp, self.tabd[1], czero[:, :], ds, reads=[rc], writes=[rt])
            self.sp.wait(rt.w)
            ds3 = self.dsem("tb3")
            for pb in (0, 8, 64, 72):
                self.dma(self.sp, self.tabd[0, pb:pb + 8, :].rearrange("f (g t) -> (f g) t", t=T), co[P, :], ds3, reads=[R["co"]], writes=[rt])
                self.dma(self.sp, self.tabd[1, pb:pb + 8, :].rearrange("f (g t) -> (f g) t", t=T), so[P, :], ds3, reads=[R["so"]], writes=[rt])
            self.drain([rt])

    def norm_tile(self, j, src, hA, rhA, sq, rsq, sd, rsd, hn, rhn, gvec, rg, dsl, onesd, rconst, nb):
        nbank = self.pbank[nb]
        rnb = self.rb[nb]
        self.dma(self.sp, hA[:, :, :], self.hview(self.src_of(src), j), dsl, reads=[self.dreg(src, j)], writes=[rhA])
        for c in range(KC):
            s2 = c % 2
            self.op(self.act, lambda h: h.activation(out=sq[:, s2, :], in_=hA[:, c, :], func=AF.Square), [rhA], [rsq[s2]])
            self.mm(nbank[:, :], onesd, sq[:, s2, :], c == 0, c == KC - 1, [rsq[s2], rconst], [rnb], True)
        self.op(self.act, lambda h: h.activation(out=sd[:, :], in_=nbank[:, :], func=AF.Sqrt, bias=self.epsb[:, 0:1], scale=1.0), [rnb, rconst], [rsd])
        self.op(self.dve, lambda h: h.reciprocal(out=nbank[:, :], in_=sd[:, :]), [rsd], [rnb])
        for c in range(KC):
            self.op(self.dve, lambda h: h.scalar_tensor_tensor(out=hn[:, c, :], in0=hA[:, c, :], scalar=gvec[:, c:c + 1], in1=nbank[:, :],
                                                               op0=ALU.mult, op1=ALU.mult), [rhA, rnb, rg], [rhn])

    def load_w(self, dst, src_rows, ds, res, nk, kstep=1, **kw):
        v = src_rows.rearrange("(c p) n -> p c n", p=128)
        for k in range(0, nk, kstep):
            self.dma(self.pool, dst[:, k:k + kstep, :], v[:, k:k + kstep, :], ds, writes=[res], max_dma_last_dim=4096)

    def phase_M(self):
        nc = self.nc
        li, lay = self.li, self.lay
        NTL = self.NTL
        kv_tiles, full_tiles = lay["kv"], lay["full"]
        tiles = list(kv_tiles) + list(full_tiles)
        fix_tile = self.cfg["fix_tile"]
        src = lay["src"]
        with ExitStack() as es:
            def sb(name, shape, dtype):
                return es.enter_context(nc.sbuf_tensor(f"m{li}_" + name, shape, dtype))
            win = sb("win", [128, KC, NIN], BF16)
            wout = sb("wout", [128, KC, D], BF16)
            wpb = sb("wpb", [128, 256], BF16)
            cm = sb("cm", [128, 768], BF16)
            vtab = sb("vtab", [128, NTL * 64], BF16)
            nrmv = sb("nrmv", [128, 24], F32)
            psc = sb("psc", [128, 2], F32)
            cst = sb("cst", [128, 34], F32)
            self.epsb = sb("epsb", [128, 1], F32)
            hA = sb("hA", [128, KC, T], F32)
            hres = hA[:, 7:8, :]
            sq = sb("sq", [128, 2, T], BF16)
            sd = sb("sd", [128, T], F32)
            hn2 = sb("hn", [128, KC, 2, T], BF16)
            tab = sb("tab", [128, 1, 2, T], F32)
            qT = sb("qT", [128, 1, 6, T], BF16)
            kr0 = sb("kr0", [128, 2, 2 * T], BF16)
            kr1 = sb("kr1", [128, 2, 2 * T], BF16)
            kr2 = sb("kr2", [128, 2, 2 * 2048], BF16)
            vr0 = sb("vr0", [128, 8, 256], BF16)
            vr1 = sb("vr1", [128, 8, 256], BF16)
            vr2 = sb("vr2", [128, 32, 256], BF16)
            U = sb("U", [128, 2, T + 16], F32)
            PA = sb("PA", [128, T + 16], F32)
            PB = sb("PB", [128, T + 16], F32)
            Y = sb("Y", [128, 2, T], BF16)
            pout = sb("pout", [128, 2, T], BF16)
            zb = sb("zb", [128, 1, T], BF16)
            t1 = hA[:, 5:6, :]
            t2 = hA[:, 6:7, :]
            E = sb("E", [128, 1, 2, T], BF16)
            numS = hA[:, 0:3, :]
            Dm = hA[:, 3, :]
            Rm = hA[:, 4, :]
            attnT = sb("attnT", [128, 6, T], BF16)
            fx = sd[:, 0:16]

            pm = cm[:, 0:128]
            onesd = cm[:, 128:256]
            m01 = cm[:, 256:512]
            m2 = cm[:, 512:768]
            invw = cst[:, 0:2]
            invcnt = cst[:, 2:34]
            g1 = nrmv[:, 0:8]

            rW, rconst, rg = Res(), Res(), Res()
            dsw = self.dsem("mw")
            pl, sp, act, dve, pe = self.pool, self.sp, self.act, self.dve, self.pe
            self.dma(pl, cm[:, :], self.cmat[:, :], dsw, writes=[rconst])
            self.dma(pl, vtab[:, :], self.vtabd[:, :], dsw, writes=[rconst])
            self.dma(pl, wpb[:, :], self.wpbd[li], dsw, writes=[rconst])
            dsws = self.dsem("mws")
            self.dma(sp, nrmv[:, :], self.nrm[li], dsws, writes=[rg])
            self.dma(sp, psc[:, :], self.pscale[li], dsws, writes=[rg])
            self.dma(sp, cst[:, :], self.cst[:, :], dsws, writes=[rg])
            self.op(pl, lambda h: h.memset(self.epsb[:, :], EPS), [], [Res()])
            self.load_w(win, self.w_in[li], dsw, rW, KC)
            self.load_w(wout, self.w_out[li], dsw, rW, KC, kstep=2)
            tot = (dsw.sem, dsw.cnt, None)
            for r in (rW, rconst):
                r.w = tot
            rg.w = (dsws.sem, dsws.cnt, None)
            rK = [Res(), Res(), Res()]
            rV = [Res(), Res(), Res()]
            rU, rPA, rPB, rY, rpout = Res(), Res(), Res(), Res(), Res()
            rtab = [Res()] * 2
            for (buf, r) in ((kr0, rK[0]), (kr1, rK[1]), (kr2, rK[2]), (vr0, rV[0]), (vr1, rV[1]), (vr2, rV[2])):
                self.op(pl, lambda h: h.memset(buf[:], 0.0), [], [r])
            self.op(pl, lambda h: h.memset(U[:], 0.0), [], [rU])

            rhA, rsd = Res(), Res()
            rfx = rsd
            rsq = [Res(), Res()]
            rhn = [Res(), Res()]
            rq = [Res()] * 2
            rzb = [Res()] * 2
            rt1 = [rhA, rhA]
            rt2 = [rhA, rhA]
            rE = [Res(), Res()]
            rE[1] = rE[0]
            rnum, rDm, rRm, rattn = [rhA, rhA, rhA], rhA, rhA, Res()
            rhres = [rhA, rhA]
            dsl = self.dsem("ml")
            dst2 = [self.dsem("mt0")] * 2
            dsr = [self.dsem("mr0")] * 2
            NB, PZ, SP_, SC_, NUM, DEN = 0, 1, 2, 3, 4, 5
            zcount = [0]

            def zbank():
                i = zcount[0] % 2
                zcount[0] += 1
                return self.pz2[:, i * T:(i + 1) * T], self.rz[i]

            def norm(j):
                par = j % 2
                self.norm_tile(j, src, hA, rhA, sq, rsq, sd, rsd, hn2[:, :, par, :], rhn[par], g1, rg, dsl, onesd, rconst, NB)
                self.dma(sp, tab[:, 0, :, :], self.tabd.rearrange("a p n -> p a n")[:, :, j * T:(j + 1) * T], dst2[par],
                         reads=[self.dreg("tab", 0)], writes=[rtab[par]])

            def proj_chunk(j, col0):
                par = j % 2
                zb_ap, rzz = zbank()
                for k in range(KC):
                    self.mm(zb_ap, win[:, k, col0:col0 + 128], hn2[:, k, par, :], k == 0, k == KC - 1,
                            [rW, rhn[par]], [rzz], k == KC - 1)
                return zb_ap, rzz

            def rotary(j, col0, slot, outs):
                par = j % 2
                z_ap, rzz = proj_chunk(j, col0)
                rstop = self.cfg.get("rstop", 99)
                self.op(act, lambda h: h.activation(out=zb[:, 0, :], in_=z_ap, func=AF.Copy), [rzz], [rzb[slot]])
                if rstop <= 1:
                    return
                pzb = self.pbank[PZ]
                self.mm(pzb[:, :], pm, zb[:, 0, :], True, True, [rzb[slot], rconst], [self.rb[PZ]], True)
                if rstop <= 2:
                    return
                self.op(dve, lambda h: h.tensor_tensor(out=t1[:, 0, :], in0=z_ap, in1=tab[:, 0, 0, :], op=ALU.mult),
                        [rzz, rtab[par]], [rt1[0]])
                if rstop <= 3:
                    return
                self.op(dve, lambda h: h.tensor_tensor(out=t2[:, 0, :], in0=pzb[:, :], in1=tab[:, 0, 1, :], op=ALU.mult),
                        [self.rb[PZ], rtab[par]], [rt2[0]])
                if rstop <= 4:
                    return
                for (o_ap, vfn, r) in outs:
                    self.op(dve, lambda h: h.tensor_tensor(out=o_ap, in0=vfn(t1[:, 0, :]), in1=vfn(t2[:, 0, :]), op=ALU.add),
                            [rt1[0], rt2[0]], [r])

            def gview(g):
                if g == 0:
                    return lambda a: a
                r_ = DILS[g]
                return lambda a: a.rearrange("p (i r) -> p r i", r=r_)

            pstop = self.cfg.get("pstop", 99)

            def proj(j, full):
                par = j % 2
                slot4 = (j % 2) * 4
                for oc in range(2):
                    z_ap, rzz = proj_chunk(j, oc * 128)
                    self.op(act, lambda h: h.activation(out=U[:, oc, 16:16 + T], in_=z_ap, func=AF.Copy), [rzz], [rU])
                if pstop <= 1:
                    return
                for kc in range(self.cfg.get("kcmax", 6)):
                    g, c = kc // 2, kc % 2
                    if g == 0:
                        o_ap = kr0[:, c, (j % 2) * T:(j % 2 + 1) * T]
                    elif g == 1:
                        o_ap = kr1[:, c, (j % 2) * T:(j % 2 + 1) * T].rearrange("p (r i) -> p r i", r=4)
                    else:
                        sp_ = (j // 4) % 2
                        qt = j % 4
                        o_ap = kr2[:, c, sp_ * 2048:(sp_ + 1) * 2048].rearrange("p (r k) -> p r k", r=16)[:, :, 32 * qt:32 * qt + 32]
                    rotary(j, 1024 + kc * 128, kc % 2, [(o_ap, gview(g), rK[g])])
                if pstop <= 2:
                    return
                if full:
                    for qc in range(6):
                        g = qc // 2
                        if g == 0:
                            o_ap = qT[:, 0, qc, :]
                        else:
                            o_ap = qT[:, 0, qc, :].rearrange("p (r i) -> p r i", r=DILS[g])
                        rotary(j, 256 + qc * 128, qc % 2, [(o_ap, gview(g), rq[par])])
                vcol = 1792
                for b in range(4):
                    for k in range(KC):
                        self.mm(self.pz2[:, b * 256:(b + 1) * 256], hn2[:, k, par, b * 128:(b + 1) * 128], win[:, k, vcol:vcol + 256],
                                k == 0, k == KC - 1, [rW, rhn[par]], [self.rz[0], self.rz[1]], k == KC - 1 and b == 3)
                self.op(act, lambda h: h.activation(out=vr0[:, slot4:slot4 + 4, :], in_=self.pz2[:, :].rearrange("p (b n) -> p b n", b=4), func=AF.Copy),
                        [self.rz[0], self.rz[1]], [rV[0]])
                if pstop <= 3:
                    return
                for r_ in range(4):
                    for k in range(KC):
                        self.mm(self.pz2[:, r_ * 256:(r_ + 1) * 256], hn2[:, k, par, r_:T:4], win[:, k, vcol + 256:vcol + 512],
                                k == 0, k == KC - 1, [rW, rhn[par]], [self.rz[0], self.rz[1]], k == KC - 1 and r_ == 3)
                self.op(act, lambda h: h.activation(out=vr1[:, slot4:slot4 + 4, :], in_=self.pz2[:, :].rearrange("p (b n) -> p b n", b=4), func=AF.Copy),
                        [self.rz[0], self.rz[1]], [rV[1]])
                if pstop <= 4:
                    return
                qt = j % 4
                sp_ = (j // 4) % 2
                if qt < 3:
                    p0, pn = 32 * qt, 32
                else:
                    p0, pn = 64, 64
                for rnd in range(4):
                    for rr in range(4):
                        r_ = rnd * 4 + rr
                        for k in range(KC):
                            if qt < 3:
                                lhsT = hn2[:, k, par, r_:T:16]
                            else:
                                lhsT = hn2[:, k, :, :].rearrange("p a t -> p (a t)")[:, r_:2 * T:16]
                            self.mm(self.pz2[p0:p0 + pn, rr * 256:(rr + 1) * 256], lhsT, win[:, k, vcol + 512:vcol + 768],
                                    k == 0, k == KC - 1, [rW, rhn[0], rhn[1]], [self.rz[0], self.rz[1]], k == KC - 1 and rr == 3)
                    self.op(act, lambda h: h.activation(out=vr2[p0:p0 + pn, sp_ * 16 + rnd * 4:sp_ * 16 + rnd * 4 + 4, :],
                                                        in_=self.pz2[p0:p0 + pn, :].rearrange("p (b n) -> p b n", b=4), func=AF.Copy),
                            [self.rz[0], self.rz[1]], [rV[2]])

            def poolmix(j, full):
                if full:
                    for c in range(2):
                        self.op(dve, lambda h: h.tensor_tensor(out=PA[:, 1:T + 16], in0=U[:, c, 1:T + 16], in1=U[:, c, 0:T + 15], op=ALU.add),
                                [rU], [rPA])
                        self.op(dve, lambda h: h.tensor_tensor(out=PB[:, 3:T + 16], in0=PA[:, 3:T + 16], in1=PA[:, 1:T + 14], op=ALU.add),
                                [rPA], [rPB])
                        if c == 1:
                            self.op(dve, lambda h: h.tensor_tensor(out=PA[:, 7:T + 16], in0=PB[:, 7:T + 16], in1=PB[:, 3:T + 12], op=ALU.add),
                                    [rPB], [rPA])
                            self.op(dve, lambda h: h.tensor_tensor(out=PB[64:128, 15:T + 16], in0=PA[64:128, 15:T + 16], in1=PA[64:128, 7:T + 8], op=ALU.add),
                                    [rPA], [rPB])
                        for (ps_, buf, rb_) in ((slice(0, 64), PA, rPA), (slice(64, 128), PB, rPB)):
                            self.op(dve, lambda h: h.scalar_tensor_tensor(out=Y[ps_, c, :], in0=buf[ps_, 16:16 + T], scalar=invw[ps_, c:c + 1],
                                                                           in1=U[ps_, c, 16:16 + T], op0=ALU.mult, op1=ALU.subtract),
                                    [rb_, rU, rg], [rY])
                            if j == fix_tile:
                                self.op(dve, lambda h: h.tensor_tensor(out=fx[ps_, :], in0=buf[ps_, 16:32],
                                                                       in1=invcnt[ps_, c * 16:(c + 1) * 16], op=ALU.mult),
                                        [rb_, rg], [rfx])
                                self.op(dve, lambda h: h.tensor_tensor(out=Y[ps_, c, 0:16], in0=fx[ps_, :],
                                                                       in1=U[ps_, c, 16:32], op=ALU.subtract),
                                        [rfx, rU], [rY])
                        pzb = self.pbank[PZ]
                        self.mm(pzb[:, :], wpb[:, c * 128:(c + 1) * 128], Y[:, c, :], True, True, [rY, rconst], [self.rb[PZ]], True)
                        self.op(act, lambda h: h.activation(out=pout[:, c, :], in_=pzb[:, :], func=AF.Identity, scale=psc[:, c:c + 1]),
                                [self.rb[PZ], rg], [rpout])
                self.op(dve, lambda h: h.tensor_copy(out=U[:, :, 0:16], in_=U[:, :, T:T + 16]), [rU], [rU])

            def kblk(g, c, pb, j, b, prev):
                if g == 0:
                    if not prev:
                        sl, bb, tj = j % 2, b, j
                    elif b > 0:
                        sl, bb, tj = j % 2, b - 1, j
                    else:
                        sl, bb, tj = (j - 1) % 2, 3, j - 1
                    return kr0[pb:pb + 64, c, sl * T + bb * 128:sl * T + bb * 128 + 128], vr0, sl * 4 + bb, tj
                if g == 1:
                    sl = (j - 1) % 2 if prev else j % 2
                    tj = j - 1 if prev else j
                    return kr1[pb:pb + 64, c, sl * T + b * 128:sl * T + b * 128 + 128], vr1, sl * 4 + b, tj
                sp_ = (j // 4) % 2
                if prev:
                    sp_ = 1 - sp_
                    tj = (j // 4) * 4 - 1
                else:
                    tj = j
                return kr2[pb:pb + 64, c, sp_ * 2048 + b * 128:sp_ * 2048 + b * 128 + 128], vr2, sp_ * 16 + b, tj

            def attention(j):
                par = j % 2
                qt = j % 4
                hcount = 0
                for c in range(2):
                    for g in range(3):
                        qc = 2 * g + c
                        nb_ = 4 if g < 2 else 16
                        nq = T // nb_
                        numb, denb = self.pbank[NUM], self.pbank[DEN]
                        for hh in range(2):
                            pb = 64 * hh
                            hl = 2 * c + hh
                            eb = hcount % 2
                            hcount += 1
                            for (prev, bank) in ((True, SP_), (False, SC_)):
                                for b in range(nb_):
                                    kap, _, _, _ = kblk(g, c, pb, j, b, prev)
                                    self.mm(self.pbank[bank][:, b * nq:(b + 1) * nq], kap, qT[pb:pb + 64, 0, qc, b * nq:(b + 1) * nq],
                                            True, True, [rK[g], rq[par]], [self.rb[bank]], b == nb_ - 1)
                            for (pi_, bank) in ((0, SP_), (1, SC_)):
                                self.op(act, lambda h: h.activation(out=E[:, 0, pi_, :], in_=self.pbank[bank][:, :], func=AF.Exp, scale=0.125),
                                        [self.rb[bank]], [rE[eb]])
                                if g < 2:
                                    mk = m01[:, (1 - pi_) * 128:(2 - pi_) * 128]
                                    mkb = mk.unsqueeze(1).broadcast_to([128, 4, 128])
                                    ev = E[:, 0, pi_, :].rearrange("p (b q) -> p b q", b=4)
                                else:
                                    mi = qt * 2 + (1 - pi_)
                                    mk = m2[:, mi * 32:(mi + 1) * 32]
                                    mkb = mk.unsqueeze(1).broadcast_to([128, 16, 32])
                                    ev = E[:, 0, pi_, :].rearrange("p (b q) -> p b q", b=16)
                                self.op(dve, lambda h: h.tensor_tensor(out=ev, in0=ev, in1=mkb, op=ALU.mult),
                                        [rE[eb], rconst], [rE[eb]])
                            for b in range(nb_):
                                for (pi_, prev) in ((0, True), (1, False)):
                                    _, vr, vi, _ = kblk(g, c, pb, j, b, prev)
                                    self.mm(numb[pb:pb + 64, b * nq:(b + 1) * nq], vr[:, vi, hl * 64:(hl + 1) * 64], E[:, 0, pi_, b * nq:(b + 1) * nq],
                                            pi_ == 0, pi_ == 1, [rV[g], rE[eb]], [self.rb[NUM]], b == nb_ - 1 and pi_ == 1)
                            if g == 0:
                                segs = [(0, 128, j - 1), (128, T, j)]
                            elif g == 1:
                                segs = [(0, T, j - 1)]
                            else:
                                segs = [(0, T, (j // 4) * 4 - 1)]
                            for si, (a0, a1, tprev) in enumerate(segs):
                                tprev = max(tprev, 0)
                                self.mm(denb[pb:pb + 64, a0:a1], vtab[:, tprev * 64:(tprev + 1) * 64], E[:, 0, 0, a0:a1], True, False,
                                        [rE[eb], rconst], [self.rb[DEN]], False)
                                self.mm(denb[pb:pb + 64, a0:a1], vtab[:, j * 64:(j + 1) * 64], E[:, 0, 1, a0:a1], False, True,
                                        [rE[eb], rconst], [self.rb[DEN]], si == len(segs) - 1)
                        gv = gview(g)
                        if g == 0:
                            n_out, n_in, d_in, d_nat = numS[:, g, :], numb[:, :], denb[:, :], Dm
                        else:
                            n_out = numS[:, g, :].rearrange("p (i r) -> p r i", r=DILS[g])
                            n_in = numb[:, :].rearrange("p (r i) -> p r i", r=DILS[g])
                            d_in = denb[:, :].rearrange("p (r i) -> p r i", r=DILS[g])
                            d_nat = Dm.rearrange("p (i r) -> p r i", r=DILS[g])
                        self.op(act, lambda h: h.activation(out=n_out, in_=n_in, func=AF.Copy), [self.rb[NUM]], [rnum[g]])
                        if g == 0:
                            self.op(dve, lambda h: h.tensor_copy(out=d_nat, in_=d_in), [self.rb[DEN]], [rDm])
                        else:
                            self.op(dve, lambda h: h.tensor_tensor(out=d_nat, in0=d_nat, in1=d_in, op=ALU.add), [self.rb[DEN], rDm], [rDm])
                    self.op(dve, lambda h: h.tensor_scalar_max(out=Dm, in0=Dm, scalar1=1e-30), [rDm], [rDm])
                    self.op(dve, lambda h: h.reciprocal(out=Rm, in_=Dm), [rDm], [rRm])
                    for g in range(3):
                        self.op(dve, lambda h: h.tensor_tensor(out=attnT[:, 2 * g + c, :], in0=numS[:, g, :], in1=Rm, op=ALU.mult),
                                [rnum[g], rRm], [rattn])

            def outproj(j):
                dstn = "hm"
                for oc in range(KC):
                    z_ap, rzz = zbank()
                    for k in range(KC):
                        rhs = pout[:, k, :] if k < 2 else attnT[:, k - 2, :]
                        self.mm(z_ap, wout[:, k, oc * 128:(oc + 1) * 128], rhs, k == 0, k == KC - 1, [rW, rpout, rattn], [rzz], k == KC - 1)
                    s_ = 0
                    srcv = self.src_of(src).rearrange("(c p) n -> p c n", p=128)[:, oc, j * T:(j + 1) * T]
                    dstv = self.hm.rearrange("(c p) n -> p c n", p=128)[:, oc, j * T:(j + 1) * T]
                    self.dma(sp, hres[:, s_, :], srcv, dsr[s_], reads=[self.dreg(src, j)], writes=[rhres[s_]])
                    self.op(dve, lambda h: h.tensor_tensor(out=hres[:, s_, :], in0=z_ap, in1=hres[:, s_, :], op=ALU.add), [rzz, rhres[s_]], [rhres[s_]])
                    self.dma(sp, dstv, hres[:, s_, :], dsr[s_], reads=[rhres[s_]], writes=[self.dreg(dstn, j)])

            mstop = self.cfg.get("mstop", 99)
            norm(tiles[0])
            for idx, j in enumerate(tiles):
                full = j in full_tiles
                if mstop <= 1:
                    break
                if mstop <= 4 and full:
                    break
                proj(j, full)
                poolmix(j, full)
                if mstop <= 3:
                    break
                if idx + 1 < len(tiles):
                    norm(tiles[idx + 1])
                if full:
                    if mstop <= 5:
                        break
                    attention(j)
                    if mstop <= 6:
                        break
                    outproj(j)
            self.drain([self.dreg("hm", j) for j in full_tiles])

    def drain(self, dregs):
        for e in (self.sp, self.pool, self.act, self.dve, self.pe):
            for r in dregs:
                if r.w is not None:
                    e.wait(r.w)
        engs = (self.pe, self.act, self.dve, self.pool)
        for e in (self.sp, self.pool, self.act, self.dve, self.pe):
            for o in engs:
                if o is not e and o.cnt > 0:
                    e.wait((o.sem, o.cnt, o))

    def phase_F(self, half):
        nc = self.nc
        li, lay = self.li, self.lay
        full_tiles = list(lay["full"])
        with ExitStack() as es:
            def sb(name, shape, dtype):
                return es.enter_context(nc.sbuf_tensor(f"f{li}{half}_" + name, shape, dtype))
            wup = sb("wup", [128, KC, DFF // 2], BF16)
            wdn = sb("wdn", [128, 16, D], BF16)
            onesb = sb("ones", [128, 128], BF16)
            nrmv = sb("nrmv", [128, 24], F32)
            self.epsb = sb("epsb", [128, 1], F32)
            hA = sb("hA", [128, KC, T], F32)
            hres = sb("hres", [128, 2, T], F32)
            sq = sb("sq", [128, 2, T], BF16)
            sd = sb("sd", [128, T], F32)
            hn = sb("hn", [128, KC, T], BF16)
            rl = sb("rl", [128, 2, T], BF16)
            aT = sb("aT", [128, 16, T], BF16)
            g2 = nrmv[:, 8:16]
            rW, rconst, rg = Res(), Res(), Res()
            dsw = self.dsem("fw")
            pl, sp, act, dve, pe = self.pool, self.sp, self.act, self.dve, self.pe
            self.dma(pl, onesb[:, :], self.cmat[:, 128:256], dsw, writes=[rconst])
            dsws = self.dsem("fws")
            self.dma(sp, nrmv[:, :], self.nrm[li], dsws, writes=[rg])
            self.op(pl, lambda h: h.memset(self.epsb[:, :], EPS), [], [Res()])
            self.load_w(wup, self.w_up[li][:, half * 2048:(half + 1) * 2048], dsw, rW, KC)
            self.load_w(wdn, self.w_down[li][half * 2048:(half + 1) * 2048, :], dsw, rW, 16, kstep=4)
            tot = (dsw.sem, dsw.cnt, None)
            for r in (rW, rconst):
                r.w = tot
            rg.w = (dsws.sem, dsws.cnt, None)
            rhA, rsd, rhn, raT = Res(), Res(), Res(), Res()
            rsq = [Res(), Res()]
            rrl = [Res(), Res()]
            rhres = [Res(), Res()]
            dsl = self.dsem("fl")
            dsr = [self.dsem("fr0"), self.dsem("fr1")]
            NB = 0
            cnt = [0]
            rsrc, rdst = ("hm", "hf") if half == 0 else ("hf", "hg")

            def bank():
                i = 1 + cnt[0] % 5
                cnt[0] += 1
                return self.pbank[i][:, :], self.rb[i]

            def norm(j):
                self.norm_tile(j, "hm", hA, rhA, sq, rsq, sd, rsd, hn, rhn, g2, rg, dsl, onesb[:, :], rconst, NB)

            def up(j):
                for oc in range(16):
                    b_ap, rbk = bank()
                    for k in range(KC):
                        self.mm(b_ap, wup[:, k, oc * 128:(oc + 1) * 128], hn[:, k, :], k == 0, k == KC - 1, [rW, rhn], [rbk], k == KC - 1)
                    s_ = oc % 2
                    self.op(act, lambda h: h.activation(out=rl[:, s_, :], in_=b_ap, func=AF.Relu), [rbk], [rrl[s_]])
                    self.op(dve, lambda h: h.tensor_tensor(out=aT[:, oc, :], in0=rl[:, s_, :], in1=rl[:, s_, :], op=ALU.mult),
                            [rrl[s_]], [raT])

            def down(j):
                for oc in range(KC):
                    b_ap, rbk = bank()
                    for k in range(16):
                        self.mm(b_ap, wdn[:, k, oc * 128:(oc + 1) * 128], aT[:, k, :], k == 0, k == 15, [rW, raT], [rbk], k == 15)
                    s_ = oc % 2
                    srcv = self.src_of(rsrc).rearrange("(c p) n -> p c n", p=128)[:, oc, j * T:(j + 1) * T]
                    dstv = self.src_of(rdst).rearrange("(c p) n -> p c n", p=128)[:, oc, j * T:(j + 1) * T]
                    self.dma(sp, hres[:, s_, :], srcv, dsr[s_], reads=[self.dreg(rsrc, j)], writes=[rhres[s_]])
                    self.op(dve, lambda h: h.tensor_tensor(out=hres[:, s_, :], in0=b_ap, in1=hres[:, s_, :], op=ALU.add), [rbk, rhres[s_]], [rhres[s_]])
                    self.dma(sp, dstv, hres[:, s_, :], dsr[s_], reads=[rhres[s_]], writes=[self.dreg(rdst, j)])

            norm(full_tiles[0])
            for idx, j in enumerate(full_tiles):
                up(j)
                if idx + 1 < len(full_tiles):
                    norm(full_tiles[idx + 1])
                down(j)
            self.drain([self.dreg(rdst, j) for j in full_tiles])

    def phase_G(self):
        nc = self.nc
        li, lay = self.li, self.lay
        full_tiles = list(lay["full"])
        final = lay["final"]
        to_out = lay["to_out"]
        out_toks = []
        with ExitStack() as es:
            def sb(name, shape, dtype):
                return es.enter_context(nc.sbuf_tensor(f"g{li}_" + name, shape, dtype))
            wg = sb("wg", [128, KC, D], BF16)
            wp = sb("wp", [128, 2, D], BF16)
            onesb = sb("ones", [128, 128], BF16)
            nrmv = sb("nrmv", [128, 24], F32)
            fnv = sb("fnv", [128, 8], F32)
            self.epsb = sb("epsb", [128, 1], F32)
            hA2 = sb("hA", [128, 2, KC, T], F32)
            sq = sb("sq", [128, 2, T], BF16)
            sd = sb("sd", [128, T], F32)
            hn2 = sb("hn", [128, 2, KC, T], BF16)
            pb2 = sb("pb", [128, 2, 2, T], BF16)
            pf2 = sb("pf", [128, 2, 2, T], F32)
            rpf = [Res(), Res()]
            sg = sb("sg", [128, 2, T], F32)
            tmp = sb("tmp", [128, 2, T], F32)
            ho = sb("ho", [128, 2, KC, T], F32)
            g3 = nrmv[:, 16:24]
            rW, rconst, rg = Res(), Res(), Res()
            dsw = self.dsem("gw")
            pl, sp, act, dve, pe = self.pool, self.sp, self.act, self.dve, self.pe
            self.dma(pl, onesb[:, :], self.cmat[:, 128:256], dsw, writes=[rconst])
            dsws = self.dsem("gws")
            self.dma(sp, nrmv[:, :], self.nrm[li], dsws, writes=[rg])
            self.dma(sp, fnv[:, :], self.fnorm[:, :], dsws, writes=[rg])
            self.op(pl, lambda h: h.memset(self.epsb[:, :], EPS), [], [Res()])
            self.load_w(wg, self.w_gate[li], dsw, rW, KC, kstep=2)
            self.load_w(wp, self.w_ple[li], dsw, rW, 2, kstep=2)
            tot = (dsw.sem, dsw.cnt, None)
            for r in (rW, rconst):
                r.w = tot
            rg.w = (dsws.sem, dsws.cnt, None)
            rhA = [Res(), Res()]
            rhn = [Res(), Res()]
            rpb = [Res(), Res()]
            rho = [Res(), Res()]
            rsd = Res()
            rsq = [Res(), Res()]
            rsg = [Res(), Res()]
            rtmp = [Res(), Res()]
            dsl = [self.dsem("gl0"), self.dsem("gl1")]
            dsp = [self.dsem("gp0"), self.dsem("gp1")]
            dso = [self.dsem("go0"), self.dsem("go1")]
            NB, NB2 = 0, 5
            cnt = [0]

            def bank():
                i = 1 + cnt[0] % 4
                cnt[0] += 1
                return self.pbank[i][:, :], self.rb[i]

            def norm(j):
                par = j % 2
                self.norm_tile(j, "hg", hA2[:, par], rhA[par], sq, rsq, sd, rsd, hn2[:, par], rhn[par], g3, rg, dsl[par], onesb[:, :], rconst, NB)
                pv = self.pT[li].rearrange("(c p) n -> p c n", p=128)[:, :, j * T:(j + 1) * T]
                self.dma(sp, pf2[:, par, :, :], pv, dsp[par], writes=[rpf[par]])
                self.op(dve, lambda h: h.tensor_copy(out=pb2[:, par, :, :], in_=pf2[:, par, :, :]), [rpf[par]], [rpb[par]])

            def gate(j):
                par = j % 2
                hA = hA2[:, par]
                for oc in range(KC):
                    g_ap, rgb = bank()
                    for k in range(KC):
                        self.mm(g_ap, wg[:, k, oc * 128:(oc + 1) * 128], hn2[:, par, k, :], k == 0, k == KC - 1, [rW, rhn[par]], [rgb], k == KC - 1)
                    p_ap, rpbk = bank()
                    for k in range(2):
                        self.mm(p_ap, wp[:, k, oc * 128:(oc + 1) * 128], pb2[:, par, k, :], k == 0, k == 1, [rW, rpb[par]], [rpbk], k == 1)
                    s_ = oc % 2
                    self.op(act, lambda h: h.activation(out=sg[:, s_, :], in_=g_ap, func=AF.Exp, scale=-1.0), [rgb], [rsg[s_]])
                    self.op(dve, lambda h: h.tensor_scalar(out=sg[:, s_, :], in0=sg[:, s_, :], scalar1=1.0, scalar2=None, op0=ALU.add), [rsg[s_]], [rsg[s_]])
                    self.op(dve, lambda h: h.reciprocal(out=sg[:, s_, :], in_=sg[:, s_, :]), [rsg[s_]], [rsg[s_]])
                    self.op(dve, lambda h: h.tensor_tensor(out=tmp[:, s_, :], in0=p_ap, in1=sg[:, s_, :], op=ALU.mult), [rpbk, rsg[s_]], [rtmp[s_]])
                    self.op(dve, lambda h: h.tensor_tensor(out=hA[:, oc, :], in0=hA[:, oc, :], in1=tmp[:, s_, :], op=ALU.add), [rtmp[s_], rhA[par]], [rhA[par]])
                if not to_out:
                    dname = lay["dst"]
                    tok = self.dma(sp, self.hview(self.src_of(dname), j), hA[:, :, :], dso[par], reads=[rhA[par]], writes=[self.dreg(dname, j)])
                elif not final:
                    jo = j - lay["out_off"]
                    tok = self.dma(sp, self.hview(self.outT, jo), hA[:, :, :], dso[par], reads=[rhA[par]], writes=[self.dreg("out", j)])
                    out_toks.append(tok)
                else:
                    nbank = self.pbank[NB2]
                    rnb = self.rb[NB2]
                    for c in range(KC):
                        s2 = c % 2
                        self.op(act, lambda h: h.activation(out=sq[:, s2, :], in_=hA[:, c, :], func=AF.Square), [rhA[par]], [rsq[s2]])
                        self.mm(nbank[:, :], onesb[:, :], sq[:, s2, :], c == 0, c == KC - 1, [rsq[s2], rconst], [rnb], True)
                    self.op(act, lambda h: h.activation(out=sd[:, :], in_=nbank[:, :], func=AF.Sqrt, bias=self.epsb[:, 0:1], scale=1.0), [rnb, rconst], [rsd])
                    self.op(dve, lambda h: h.reciprocal(out=nbank[:, :], in_=sd[:, :]), [rsd], [rnb])
                    for c in range(KC):
                        self.op(dve, lambda h: h.scalar_tensor_tensor(out=ho[:, par, c, :], in0=hA[:, c, :], scalar=fnv[:, c:c + 1], in1=nbank[:, :],
                                                                       op0=ALU.mult, op1=ALU.mult), [rhA[par], rnb, rg], [rho[par]])
                    jo = j - lay["out_off"]
                    tok = self.dma(sp, self.hview(self.outT, jo), ho[:, par, :, :], dso[par], reads=[rho[par]], writes=[self.dreg("out", j)])
                    out_toks.append(tok)

            norm(full_tiles[0])
            for idx, j in enumerate(full_tiles):
                if idx + 1 < len(full_tiles):
                    norm(full_tiles[idx + 1])
                gate(j)
            dn = "out" if to_out else lay["dst"]
            self.drain([self.dreg(dn, j) for j in full_tiles])
        return out_toks


def build_program(cfg):
    nc = bass.Bass("TRN2", target_bir_lowering=False)
    Builder(nc, cfg).emit()
    return nc


def _consts(ntile):
    invw = np.zeros((128, 2), np.float32)
    wins = (2, 4, 8, 16)
    invcnt = np.zeros((128, 2, 16), np.float32)
    for c in range(2):
        for a in range(2):
            invw[64 * a:64 * a + 64, c] = 1.0 / wins[2 * c + a]
    pm = np.zeros((128, 128), np.float32)
    for hb in (0, 64):
        for i in range(8):
            pm[hb + i + 8, hb + i] = -1.0
            pm[hb + i, hb + i + 8] = 1.0
    onesd = np.full((128, 128), 1.0 / D, np.float32)
    kl = np.arange(128)[:, None]
    ql = np.arange(128)[None, :]
    m01 = np.stack([(ql >= kl), (ql <= kl)], axis=1).astype(np.float32)
    m2 = np.zeros((128, 8, 32), np.float32)
    for qt in range(4):
        q2 = 32 * qt + np.arange(32)[None, :]
        m2[:, 2 * qt + 0, :] = (kl <= q2)
        m2[:, 2 * qt + 1, :] = (kl >= q2)
    cmat = np.concatenate([pm, onesd, m01.reshape(128, 256), m2.reshape(128, 256)], axis=1)
    return invw, cmat


def _invfreq():
    return (np.float32(500000.0) ** (-(np.arange(0, 16, 2, dtype=np.float32) / np.float32(16.0)))).astype(np.float32)


def _chunkvec(v):
    return np.ascontiguousarray(v.reshape(-1, 128).T.astype(np.float32))


def _core_maps(x_seq, p_seq, pos_seq, start, ntile, nmain, W, layer_ids, tok_valid_from=0):
    S = x_seq.shape[0]
    ntok = ntile * T
    lo = start + nmain * T - ntok
    xT = np.zeros((D, ntok), np.float32)
    pT = np.zeros((len(layer_ids), 256, ntok), np.float32)
    pos = np.zeros((ntok,), np.int32)
    a = max(lo, 0)
    b = start + nmain * T
    xT[:, a - lo:] = x_seq[a:b].T
    for i, l in enumerate(layer_ids):
        pT[i][:, a - lo:] = p_seq[l, a:b].T
    pos[a - lo:] = pos_seq[a:b]
    posr = np.ascontiguousarray(np.broadcast_to(pos.reshape(1, ntile, T), (8, ntile, T)).reshape(8 * ntile, T))
    invfr = np.ascontiguousarray(np.repeat(_invfreq(), ntile).reshape(8 * ntile, 1).astype(np.float32))
    invw, cmat = _consts(ntile)
    wins = (2, 4, 8, 16)
    invcnt = np.zeros((128, 2, 16), np.float32)
    fix_tok = ntok - nmain * T
    for c in range(2):
        for a_ in range(2):
            w = wins[2 * c + a_]
            tt = start + np.arange(16)
            invcnt[64 * a_:64 * a_ + 64, c, :] = 1.0 / np.minimum(tt + 1, w).astype(np.float32)
    cst = np.concatenate([invw, invcnt.reshape(128, 32)], axis=1).astype(np.float32)
    vt = np.zeros((128, ntile, 64), np.float32)
    for j in range(ntile):
        if lo + j * T >= 0:
            vt[:, j, :] = 1.0
    m = {
        "xT": xT, "pT": pT, "posr": posr, "invfr": invfr, "cst": cst, "cmat": cmat.astype(np.float32),
        "vtab": vt.reshape(128, ntile * 64),
    }
    m.update(W)
    return m


def _weights(inp, layer_ids):
    L = list(layer_ids)
    f = lambda k: np.ascontiguousarray(np.asarray(inp[k], np.float32)[L])
    W = {k: f(k) for k in ("w_in", "w_out", "w_up", "w_down", "w_gate", "w_ple")}
    pw = np.asarray(inp["pool_w"], np.float32)
    wpbd = np.zeros((len(L), 128, 2, 128), np.float32)
    for i, l in enumerate(L):
        for c in range(2):
            for a in range(2):
                wpbd[i, 64 * a:64 * a + 64, c, 64 * a:64 * a + 64] = pw[l, 2 * c + a]
    W["wpbd"] = wpbd.reshape(len(L), 128, 256)
    nrm = np.zeros((len(L), 128, 3, 8), np.float32)
    for i, l in enumerate(L):
        for t, k in enumerate(("norm1", "norm2", "norm3")):
            nrm[i, :, t, :] = _chunkvec(np.asarray(inp[k])[l])
    W["nrm"] = nrm.reshape(len(L), 128, 24)
    W["fnorm"] = _chunkvec(np.asarray(inp["final_norm"]))
    W["pscale"] = np.stack([_chunkvec(np.asarray(inp["pool_scale"])[l]) for l in L], axis=0)
    return W


_PROG_CACHE = {}


def _get_prog(key, cfg):
    if key not in _PROG_CACHE:
        _PROG_CACHE[key] = build_program(cfg)
    return _PROG_CACHE[key]


FUSED = True


def kernel(x, p, positions, norm1, w_in, pool_w, pool_scale, w_out, norm2, w_up, w_down,
           norm3, w_gate, w_ple, final_norm):
    inp = dict(x=x, p=p, positions=positions, norm1=norm1, w_in=w_in, pool_w=pool_w, pool_scale=pool_scale,
               w_out=w_out, norm2=norm2, w_up=w_up, w_down=w_down, norm3=norm3, w_gate=w_gate, w_ple=w_ple,
               final_norm=final_norm)
    x = np.asarray(x, np.float32)
    p = np.asarray(p, np.float32)
    positions = np.asarray(positions)
    Bn, S, _ = x.shape
    NMAIN = 8
    ncore = 8
    half = S // 2
    if FUSED:
        cfg = dict(ntile=16, nout=8, fix_tile=8, layers=[
            dict(kv=range(0, 4), full=range(4, 16), src="x", dst="h1", final=False, to_out=False, out_off=0),
            dict(kv=range(4, 8), full=range(8, 16), src="h1", dst=None, final=True, to_out=True, out_off=8)])
        nc = _get_prog("fused", cfg)
        W = _weights(inp, [0, 1])
        maps = []
        for core in range(ncore):
            b, hf = core // 2, core % 2
            maps.append(_core_maps(x[b], p[:, b], positions[b], hf * half, 16, NMAIN, W, [0, 1]))
        res = run_bass_kernel_spmd(nc, maps, core_ids=list(range(ncore)))
        out = np.zeros((Bn, S, D), np.float32)
        for core in range(ncore):
            b, hf = core // 2, core % 2
            out[b, hf * half:(hf + 1) * half] = res.results[core]["outT"].T
        return out
    cur = x
    for l in range(2):
        last = (l == 1)
        cfg = dict(ntile=12, nout=8, fix_tile=4, layers=[
            dict(kv=range(0, 4), full=range(4, 12), src="x", dst=None, final=last, to_out=True, out_off=4)])
        nc = _get_prog(("layer", last), cfg)
        W = _weights(inp, [l])
        maps = []
        for core in range(ncore):
            b, hf = core // 2, core % 2
            maps.append(_core_maps(cur[b], p[:, b], positions[b], hf * half, 12, NMAIN, W, [l]))
        res = run_bass_kernel_spmd(nc, maps, core_ids=list(range(ncore)))
        nxt = np.zeros((Bn, S, D), np.float32)
        for core in range(ncore):
            b, hf = core // 2, core % 2
            nxt[b, hf * half:(hf + 1) * half] = res.results[core]["outT"].T
        cur = nxt
    return cur
```
